# Optimizing a Trainium2 kernel written in Bass

```python
import math
import jax, jax.numpy as jnp
from jax import lax
import numpy as np

D_MODEL = 1024
BATCH = 4
SEQ = 4096
DEPTH = 4

HEAD_DIM = 128
N_MIX_HEADS = D_MODEL // HEAD_DIM
N_DN_HEADS = N_MIX_HEADS // 2
N_FOX_HEADS = N_MIX_HEADS - N_DN_HEADS
N_SB_HEADS = N_MIX_HEADS
D_DN = N_DN_HEADS * HEAD_DIM
D_FOX = N_FOX_HEADS * HEAD_DIM
CONV_WIDTH = 4
CHUNK = 64
Q_BLOCK = 128
D_FF = 2816
N_EVEN = (DEPTH + 1) // 2
N_ODD = DEPTH // 2
EPS = 1e-6
EVEN_SPLITS = (3 * D_DN, D_DN, N_DN_HEADS, N_DN_HEADS, D_FOX, D_FOX, D_FOX, D_FOX, N_FOX_HEADS)
D_IN_EVEN = 3 * D_DN + D_DN + 2 * N_DN_HEADS + 4 * D_FOX + N_FOX_HEADS
D_IN_ODD = 3 * N_SB_HEADS * HEAD_DIM

kernel_name = 'hybrid_deltanet_fox_stickbreak_macaron'


def _split(t, sizes):
    out, off = [], 0
    for s in sizes:
        out.append(t[..., off:off + s])
        off += s
    return out


def rmsnorm(x, g):
    xf = x.astype(jnp.float32)
    y = xf * lax.rsqrt(jnp.mean(xf * xf, axis=-1, keepdims=True) + EPS)
    return (y * g.astype(jnp.float32)).astype(x.dtype)


def l2norm(t):
    tf = t.astype(jnp.float32)
    return tf * lax.rsqrt(jnp.sum(tf * tf, axis=-1, keepdims=True) + EPS)


def swiglu_ffn(x, w_gu, w_down):
    g, u = jnp.split(x @ w_gu, 2, axis=-1)
    return (jax.nn.silu(g) * u) @ w_down


def split_heads(t, n):
    b, s, _ = t.shape
    return t.reshape(b, s, n, -1).transpose(0, 2, 1, 3)


def merge_heads(t):
    b, n, s, d = t.shape
    return t.transpose(0, 2, 1, 3).reshape(b, s, n * d)


def causal_conv_silu(x, w):
    s = x.shape[1]
    xp = jnp.pad(x, ((0, 0), (CONV_WIDTH - 1, 0), (0, 0)))
    y = sum(xp[:, i:i + s, :] * w[i] for i in range(CONV_WIDTH))
    return jax.nn.silu(y)


def gated_delta_rule(q, k, v, beta, g):
    b, h, s, d = q.shape
    n = s // CHUNK
    q = q * d ** -0.5
    rc = lambda t: t.reshape(b, h, n, CHUNK, *t.shape[3:])
    q, k, v, beta, g = (rc(t) for t in (q, k, v, beta, g))
    gc = jnp.cumsum(g, axis=-1)
    tri_incl = jnp.tril(jnp.ones((CHUNK, CHUNK), bool))
    tri_strict = jnp.tril(jnp.ones((CHUNK, CHUNK), bool), -1)
    decay_mat = jnp.where(tri_incl, jnp.exp(jnp.where(tri_incl, gc[..., :, None] - gc[..., None, :], 0.0)), 0.0)
    k_beta = k * beta[..., None]
    v_beta = v * beta[..., None]
    a = jnp.where(tri_strict, jnp.einsum('bhnid,bhnjd->bhnij', k_beta, k) * decay_mat, 0.0)
    lhs = jnp.eye(CHUNK, dtype=jnp.float32) + a
    rhs = jnp.concatenate([v_beta, k_beta * jnp.exp(gc)[..., None]], axis=-1)
    sol = lax.linalg.triangular_solve(lhs, rhs, left_side=True, lower=True)
    u, w = sol[..., :d], sol[..., d:]
    attn_intra = jnp.einsum('bhnid,bhnjd->bhnij', q, k) * decay_mat
    q_dec = q * jnp.exp(gc)[..., None]
    k_dec = k * jnp.exp(gc[..., -1:] - gc)[..., None]
    g_last = jnp.exp(gc[..., -1])

    def step(state, inp):
        u_c, w_c, qd_c, kd_c, at_c, gl_c = inp
        v_new = u_c - jnp.einsum('bhcd,bhde->bhce', w_c, state)
        o = jnp.einsum('bhcd,bhde->bhce', qd_c, state) + jnp.einsum('bhij,bhje->bhie', at_c, v_new)
        state = state * gl_c[..., None, None] + jnp.einsum('bhcd,bhce->bhde', kd_c, v_new)
        return state, o

    xs = tuple(jnp.moveaxis(t, 2, 0) for t in (u, w, q_dec, k_dec, attn_intra, g_last))
    _, o = lax.scan(step, jnp.zeros((b, h, d, d), jnp.float32), xs)
    return jnp.moveaxis(o, 0, 2).reshape(b, h, s, d)


def forgetting_attention(q, k, v, log_f):
    b, h, s, d = q.shape
    nb = s // Q_BLOCK
    c = jnp.cumsum(log_f, axis=-1)
    qb = q.reshape(b, h, nb, Q_BLOCK, d).transpose(2, 0, 1, 3, 4)
    cb = c.reshape(b, h, nb, Q_BLOCK).transpose(2, 0, 1, 3)
    pos_k = jnp.arange(s)

    def block(args):
        i, q_i, c_i = args
        pos_q = i * Q_BLOCK + jnp.arange(Q_BLOCK)
        logits = jnp.einsum('bhqd,bhkd->bhqk', q_i, k).astype(jnp.float32) * d ** -0.5
        logits = logits + (c_i[..., :, None] - c[..., None, :])
        logits = jnp.where(pos_k[None, :] <= pos_q[:, None], logits, -jnp.inf)
        p = jax.nn.softmax(logits, axis=-1)
        return jnp.einsum('bhqk,bhkd->bhqd', p.astype(v.dtype), v)

    o = lax.map(block, (jnp.arange(nb), qb, cb))
    return o.transpose(1, 2, 0, 3, 4).reshape(b, h, s, d)


def stick_breaking_attention(q, k, v):
    b, h, s, d = q.shape
    nb = s // Q_BLOCK
    qb = q.reshape(b, h, nb, Q_BLOCK, d).transpose(2, 0, 1, 3, 4)
    pos_k = jnp.arange(s)

    def block(args):
        i, q_i = args
        pos_q = i * Q_BLOCK + jnp.arange(Q_BLOCK)
        z = jnp.einsum('bhqd,bhkd->bhqk', q_i, k).astype(jnp.float32) * d ** -0.5
        before = pos_k[None, :] < pos_q[:, None]
        log_1m = jnp.where(before, jax.nn.log_sigmoid(-z), 0.0)
        tail = lax.cumsum(log_1m, axis=3, reverse=True) - log_1m
        a = jnp.where(before, jnp.exp(jax.nn.log_sigmoid(z) + tail), 0.0)
        return jnp.einsum('bhqk,bhkd->bhqd', a.astype(v.dtype), v)

    o = lax.map(block, (jnp.arange(nb), qb))
    return o.transpose(1, 2, 0, 3, 4).reshape(b, h, s, d)


def deltanet_fox_mixer(h, w_in, conv_w, a_log, dt_bias, dn_norm_g, q_norm_g, k_norm_g, f_bias, w_out):
    dn_qkv, dn_gate, dn_b, dn_a, fq, fk, fv, f_gate, f_pre = _split(h @ w_in, EVEN_SPLITS)
    dq, dk, dv = jnp.split(causal_conv_silu(dn_qkv, conv_w), 3, axis=-1)
    dq = l2norm(split_heads(dq, N_DN_HEADS))
    dk = l2norm(split_heads(dk, N_DN_HEADS))
    dv = split_heads(dv, N_DN_HEADS).astype(jnp.float32)
    beta = jax.nn.sigmoid(dn_b.astype(jnp.float32)).transpose(0, 2, 1)
    g = (-jnp.exp(a_log.astype(jnp.float32)) * jax.nn.softplus(dn_a.astype(jnp.float32) + dt_bias.astype(jnp.float32))).transpose(0, 2, 1)
    o_dn = gated_delta_rule(dq, dk, dv, beta, g).astype(h.dtype)
    o_dn = merge_heads(rmsnorm(o_dn, dn_norm_g)) * jax.nn.silu(dn_gate)
    fq = rmsnorm(split_heads(fq, N_FOX_HEADS), q_norm_g)
    fk = rmsnorm(split_heads(fk, N_FOX_HEADS), k_norm_g)
    fv = split_heads(fv, N_FOX_HEADS)
    log_f = jax.nn.log_sigmoid(f_pre.astype(jnp.float32) + f_bias.astype(jnp.float32)).transpose(0, 2, 1)
    o_fox = merge_heads(forgetting_attention(fq, fk, fv, log_f)) * jax.nn.sigmoid(f_gate)
    return jnp.concatenate([o_dn, o_fox], axis=-1) @ w_out


def stick_breaking_mixer(h, w_in, w_out):
    q, k, v = jnp.split(h @ w_in, 3, axis=-1)
    o = stick_breaking_attention(split_heads(q, N_SB_HEADS), split_heads(k, N_SB_HEADS), split_heads(v, N_SB_HEADS))
    return merge_heads(o) @ w_out


def setup_inputs(seed: int = 0) -> dict:
    key = jax.random.key(seed)
    ks = jax.random.split(key, 20)
    f32 = jnp.float32

    def w(k, shape, fan_in):
        return jax.random.normal(k, shape, f32) * fan_in ** -0.5

    def gain(k, shape):
        return 1.0 + 0.02 * jax.random.normal(k, shape, f32)

    dt = jnp.exp(jax.random.uniform(ks[8], (N_EVEN, N_DN_HEADS), f32, math.log(1e-3), math.log(1e-1)))
    return {
        'x': jax.random.normal(ks[0], (BATCH, SEQ, D_MODEL), f32),
        'norm_ffn1': gain(ks[1], (DEPTH, D_MODEL)),
        'ffn1_w_gu': w(ks[2], (DEPTH, D_MODEL, 2 * D_FF), D_MODEL),
        'ffn1_w_down': w(ks[3], (DEPTH, D_FF, D_MODEL), D_FF),
        'norm_mix': gain(ks[4], (DEPTH, D_MODEL)),
        'w_in_even': w(ks[5], (N_EVEN, D_MODEL, D_IN_EVEN), D_MODEL),
        'dn_conv_w': w(ks[6], (N_EVEN, CONV_WIDTH, 3 * D_DN), CONV_WIDTH),
        'dn_a_log': jnp.log(jax.random.uniform(ks[7], (N_EVEN, N_DN_HEADS), f32, 1.0, 16.0)),
        'dn_dt_bias': dt + jnp.log(-jnp.expm1(-dt)),
        'dn_norm_g': gain(ks[9], (N_EVEN, HEAD_DIM)),
        'fox_q_norm_g': gain(ks[10], (N_EVEN, HEAD_DIM)),
        'fox_k_norm_g': gain(ks[11], (N_EVEN, HEAD_DIM)),
        'fox_f_bias': jax.random.uniform(ks[12], (N_EVEN, N_FOX_HEADS), f32, 1.0, 4.0),
        'w_out_even': w(ks[13], (N_EVEN, D_DN + D_FOX, D_MODEL), D_DN + D_FOX),
        'w_in_odd': w(ks[14], (N_ODD, D_MODEL, D_IN_ODD), D_MODEL),
        'w_out_odd': w(ks[15], (N_ODD, N_SB_HEADS * HEAD_DIM, D_MODEL), N_SB_HEADS * HEAD_DIM),
        'norm_ffn2': gain(ks[16], (DEPTH, D_MODEL)),
        'ffn2_w_gu': w(ks[17], (DEPTH, D_MODEL, 2 * D_FF), D_MODEL),
        'ffn2_w_down': w(ks[18], (DEPTH, D_FF, D_MODEL), D_FF),
    }


def reference(x, norm_ffn1, ffn1_w_gu, ffn1_w_down, norm_mix, w_in_even, dn_conv_w, dn_a_log,
              dn_dt_bias, dn_norm_g, fox_q_norm_g, fox_k_norm_g, fox_f_bias, w_out_even,
              w_in_odd, w_out_odd, norm_ffn2, ffn2_w_gu, ffn2_w_down):
    for l in range(DEPTH):
        x = x + 0.5 * swiglu_ffn(rmsnorm(x, norm_ffn1[l]), ffn1_w_gu[l], ffn1_w_down[l])
        h = rmsnorm(x, norm_mix[l])
        j = l // 2
        if l % 2 == 0:
            x = x + deltanet_fox_mixer(h, w_in_even[j], dn_conv_w[j], dn_a_log[j], dn_dt_bias[j],
                                       dn_norm_g[j], fox_q_norm_g[j], fox_k_norm_g[j], fox_f_bias[j],
                                       w_out_even[j])
        else:
            x = x + stick_breaking_mixer(h, w_in_odd[j], w_out_odd[j])
        x = x + 0.5 * swiglu_ffn(rmsnorm(x, norm_ffn2[l]), ffn2_w_gu[l], ffn2_w_down[l])
    return x
```

```python
import numpy as np
from contextlib import ExitStack
import concourse.bass as bass
import concourse.mybir as mybir
from concourse.bass_utils import run_bass_kernel_spmd
import ml_dtypes

F32 = mybir.dt.float32
BF16 = mybir.dt.bfloat16
AF = mybir.ActivationFunctionType
ALU = mybir.AluOpType

D_MODEL = 1024
SEQ = 4096
BATCH = 4
DEPTH = 4
D_FF = 2816
TOK = 2048
EPS = 1e-6
NEG = -30000.0
SIG_WRAP = 8192


class Dep:
    __slots__ = ("name", "last_w", "readers", "sem", "cnt")

    def __init__(self, name=""):
        self.name = name
        self.last_w = None
        self.readers = []
        self.sem = None
        self.cnt = 0


class Op:
    __slots__ = ("eng", "fn", "waits", "signal", "sig", "is_dma", "dsem", "dval")

    def __init__(self, eng, fn):
        self.eng = eng
        self.fn = fn
        self.waits = []
        self.signal = False
        self.sig = None
        self.is_dma = False
        self.dsem = None
        self.dval = 0


class Builder:
    ENGS = ("pe", "act", "dve", "pool", "sp")

    def __init__(self, nc):
        self.nc = nc
        self.es = ExitStack()
        self.ops = {e: [] for e in self.ENGS}
        self.nsem = 0
        self.ntile = 0

    def sem(self):
        self.nsem += 1
        return self.es.enter_context(self.nc.semaphore("s%d" % self.nsem))

    def sb(self, shape, dt, name=None):
        self.ntile += 1
        return self.es.enter_context(self.nc.sbuf_tensor("%s_%d" % (name or "t", self.ntile), list(shape), dt))

    def ps(self, shape, dt, name=None):
        self.ntile += 1
        return self.es.enter_context(self.nc.psum_tensor("%s_%d" % (name or "p", self.ntile), list(shape), dt))

    def _dep_on(self, op, prod):
        if prod is None or prod is op:
            return
        if prod.is_dma:
            op.waits.append(prod)
        elif prod.eng != op.eng or op.eng in ("act", "dve", "pool"):
            prod.signal = True
            op.waits.append(prod)

    def op(self, eng, fn, reads=(), writes=()):
        o = Op(eng, fn)
        for d in reads:
            self._dep_on(o, d.last_w)
        for d in writes:
            self._dep_on(o, d.last_w)
            for r in d.readers:
                self._dep_on(o, r)
        for d in writes:
            d.last_w = o
            d.readers = []
        for d in reads:
            d.readers.append(o)
        self.ops[eng].append(o)
        return o

    def dma(self, eng, out, in_, reads=(), writes=(), semdep=None):
        o = self.op(eng, lambda e: e.dma_start(out=out, in_=in_), reads, writes)
        o.is_dma = True
        sd = semdep if semdep is not None else (writes[0] if writes else reads[0])
        if sd.sem is None or sd.cnt >= 1500:
            sd.sem = self.sem()
            sd.cnt = 0
        sd.cnt += 1
        o.dsem = sd.sem
        o.dval = 16 * sd.cnt
        return o

    def emit(self, final_waits=()):
        nc = self.nc
        esems = {}
        for e in self.ENGS:
            n = 0
            for o in self.ops[e]:
                if o.signal and not o.is_dma:
                    grp = n // SIG_WRAP
                    if (e, grp) not in esems:
                        esems[(e, grp)] = self.sem()
                    o.sig = (esems[(e, grp)], n % SIG_WRAP + 1)
                    n += 1
        ops = self.ops

        def run(e, eng):
            waited = {}
            for o in ops[e]:
                for p in o.waits:
                    s, v = (p.dsem, p.dval) if p.is_dma else p.sig
                    if waited.get(id(s), 0) >= v:
                        continue
                    eng.wait_ge(s, v)
                    waited[id(s)] = v
                ins = o.fn(eng)
                if o.is_dma:
                    ins.then_inc(o.dsem, 16)
                elif o.signal:
                    ins.then_inc(o.sig[0], 1)
            if e == "sp":
                for p in final_waits:
                    s, v = (p.dsem, p.dval) if p.is_dma else p.sig
                    eng.wait_ge(s, v)

        with nc.Block() as block:
            @block.tensor
            def _(t):
                run("pe", t)

            @block.scalar
            def _(t):
                run("act", t)

            @block.vector
            def _(t):
                run("dve", t)

            @block.gpsimd
            def _(t):
                run("pool", t)

            @block.sync
            def _(t):
                run("sp", t)
        self.es.close()


def wblocks(W):
    K, N = W.shape
    return np.ascontiguousarray(W.reshape(K // 128, 128, N // 128, 128).transpose(2, 1, 0, 3).reshape(N // 128, 128, K))


def gu_blocks(w_gu):
    wb = wblocks(w_gu)
    out = np.empty_like(wb)
    out[0::2] = wb[:22]
    out[1::2] = wb[22:]
    return out


def dn_blocks(w_down):
    return np.ascontiguousarray(w_down.reshape(2, 11, 128, 8, 128).transpose(0, 3, 2, 1, 4).reshape(2, 8, 128, 1408))


def gain_lay(g):
    return np.ascontiguousarray(g.reshape(8, 128).T)


def xT_lay(xs):
    T = xs.shape[0]
    return np.ascontiguousarray(xs.reshape(T, 8, 128).transpose(2, 1, 0))


def xT_unlay(a):
    T = a.shape[2]
    return np.ascontiguousarray(a.transpose(2, 1, 0).reshape(T, 1024))


def build_tl(has_out, n_ffn, fm_blocks, tm_blocks, fm_dt, tm_dt):
    nc = bass.Bass("TRN2", target_bir_lowering=False)
    B = Builder(nc)
    n_in = len(fm_blocks) + len(tm_blocks)
    n_gain = n_ffn + (1 if n_in else 0)
    xT_d = nc.dram_tensor("xT", [128, 8, TOK], F32, kind="ExternalInput").ap()
    gains_d = nc.dram_tensor("gains", [128, n_gain * 8], F32, kind="ExternalInput").ap()
    wgu_d = nc.dram_tensor("wgu", [n_ffn * 44, 128, 1024], F32, kind="ExternalInput").ap()
    wdn_d = nc.dram_tensor("wdn", [n_ffn * 16, 128, 1408], F32, kind="ExternalInput").ap()
    xo_d = nc.dram_tensor("xo", [128, 8, TOK], F32, kind="ExternalOutput").ap()
    if has_out:
        oT_d = nc.dram_tensor("oT", [8, 128, TOK], BF16, kind="ExternalInput").ap()
        wout_d = nc.dram_tensor("wout", [8, 128, 1024], F32, kind="ExternalInput").ap()
    if n_in:
        win_d = nc.dram_tensor("win", [n_in, 128, 1024], F32, kind="ExternalInput").ap()
    fm_out = []
    for i, nco in enumerate(fm_blocks):
        fm_out.append(nc.dram_tensor("fm%d" % i, [nco, TOK], F32 if fm_dt[i] == "f" else BF16, kind="ExternalOutput").ap())
    tm_out = []
    for i, nco in enumerate(tm_blocks):
        tm_out.append(nc.dram_tensor("tm%d" % i, [TOK, nco], F32 if tm_dt[i] == "f" else BF16, kind="ExternalOutput").ap())

    X = B.sb([128, 8, TOK], F32, "X")
    H = B.sb([128, 8, TOK], BF16, "H")
    ACT = B.sb([128, 11, TOK], BF16, "ACT")
    SG = B.sb([128, TOK], F32, "SG")
    RS = B.sb([128, TOK], F32, "RS")
    OSb = B.sb([128, TOK], BF16, "OSb")
    G = B.sb([128, n_gain * 8], F32, "G")
    ONES = B.sb([128, 128], BF16, "ONES")
    NWS = 2
    NWB = 3
    WS = [B.sb([128, 1408], F32, "WS") for _ in range(NWS)]
    WB = [B.sb([128, 1408], BF16, "WB") for _ in range(NWB)]
    PA = B.ps([128, TOK], F32, "PA")
    PB = B.ps([128, TOK], F32, "PB")
    dX = [Dep("X%d" % c) for c in range(8)]
    dH = Dep("H")
    dHc = [Dep("H%d" % c) for c in range(8)]
    dACT = [Dep("ACT%d" % c) for c in range(11)]
    dSG, dRS, dOSb, dG, dONES = Dep("SG"), Dep("RS"), Dep("OSb"), Dep("G"), Dep("ONES")
    dWS = [Dep("WS%d" % i) for i in range(NWS)]
    dWB = [Dep("WB%d" % i) for i in range(NWB)]
    dPA, dPB = Dep("PA"), Dep("PB")
    dOut = Dep("out")
    wctr = [0]
    dq = [0]

    def dmaq():
        dq[0] += 1
        return "sp" if dq[0] % 2 else "pool"

    EPSB = B.sb([128, 1], F32, "EPSB")
    B.op("pool", lambda e: e.memset(ONES[:], 1.0), writes=[dONES])
    B.op("pool", lambda e: e.memset(EPSB[:], EPS), writes=[dONES])
    B.dma("sp", G[:], gains_d, writes=[dG])
    for c in range(8):
        B.dma("sp", X[:, c, :], xT_d[:, c, :], writes=[dX[c]])

    def load_w(src, F):
        i = wctr[0]
        wctr[0] += 1
        ws, dws = WS[i % NWS], dWS[i % NWS]
        wb, dwb = WB[i % NWB], dWB[i % NWB]
        B.dma("sp", ws[:, :F], src, writes=[dws])
        B.op("pool", lambda e: e.tensor_copy(out=wb[:, :F], in_=ws[:, :F]), reads=[dws], writes=[dwb])
        return wb, dwb

    def mm_fm(P, dP, wb, dwb, nkc, rhs_fn, rdeps, M=128):
        for kc in range(nkc):
            for tt in range(4):
                B.op("pe", (lambda e, kc=kc, tt=tt: e.matmul(P[:M, tt * 512:(tt + 1) * 512],
                                                           lhsT=wb[:, kc * 128:kc * 128 + M],
                                                           rhs=rhs_fn(kc, tt), start=(kc == 0), stop=(kc == nkc - 1))),
                     reads=[dwb] + rdeps(kc), writes=[dP])

    def rmsnorm(gi):
        SQ = ACT
        for c in range(8):
            B.op("act", lambda e, c=c: e.activation(out=SQ[:, c, :], in_=X[:, c, :], func=AF.Square),
                 reads=[dX[c]], writes=[dACT[c]])
        for c in range(8):
            for tt in range(4):
                B.op("pe", lambda e, c=c, tt=tt: e.matmul(PA[:, tt * 512:(tt + 1) * 512], lhsT=ONES[:],
                                                          rhs=SQ[:, c, tt * 512:(tt + 1) * 512],
                                                          start=(c == 0), stop=(c == 7)),
                     reads=[dONES, dACT[c]], writes=[dPA])
        B.op("act", lambda e: e.activation(out=RS[:], in_=PA[:], func=AF.Sqrt, bias=EPSB[:], scale=1.0 / D_MODEL),
             reads=[dPA, dONES], writes=[dRS])
        B.op("dve", lambda e: e.reciprocal(out=RS[:], in_=RS[:]), reads=[dRS], writes=[dRS])
        for c in range(8):
            B.op("dve", lambda e, c=c: e.scalar_tensor_tensor(out=H[:, c, :], in0=X[:, c, :],
                                                              scalar=G[:, gi * 8 + c:gi * 8 + c + 1], in1=RS[:],
                                                              op0=ALU.mult, op1=ALU.mult),
                 reads=[dX[c], dRS, dG], writes=[dHc[c]])

    if has_out:
        for c in range(8):
            B.dma("pool", H[:, c, :], oT_d[c], writes=[dHc[c]])
        for m in range(8):
            wb, dwb = load_w(wout_d[m], 1024)
            P, dP = (PA, dPA) if m % 2 == 0 else (PB, dPB)
            mm_fm(P, dP, wb, dwb, 8, lambda kc, tt: H[:, kc, tt * 512:(tt + 1) * 512], lambda kc: [dHc[kc]])
            B.op("dve", lambda e, m=m, P=P: e.tensor_tensor(out=X[:, m, :], in0=P[:], in1=X[:, m, :], op=ALU.add),
                 reads=[dP, dX[m]], writes=[dX[m]])

    for f in range(n_ffn):
        rmsnorm(f)
        for half in range(2):
            for c in range(11):
                bi = f * 44 + (half * 11 + c) * 2
                wg, dwg = load_w(wgu_d[bi], 1024)
                mm_fm(PA, dPA, wg, dwg, 8, lambda kc, tt: H[:, kc, tt * 512:(tt + 1) * 512], lambda kc: [dHc[kc]])
                wu, dwu = load_w(wgu_d[bi + 1], 1024)
                B.op("act", lambda e: e.activation(out=SG[:], in_=PA[:], func=AF.Silu), reads=[dPA], writes=[dSG])
                mm_fm(PB, dPB, wu, dwu, 8, lambda kc, tt: H[:, kc, tt * 512:(tt + 1) * 512], lambda kc: [dHc[kc]])
                B.op("dve", lambda e, c=c: e.tensor_tensor(out=ACT[:, c, :], in0=SG[:], in1=PB[:], op=ALU.mult),
                     reads=[dSG, dPB], writes=[dACT[c]])
            for m in range(8):
                wd, dwd = load_w(wdn_d[f * 16 + half * 8 + m], 1408)
                P, dP = (PA, dPA) if m % 2 == 0 else (PB, dPB)
                mm_fm(P, dP, wd, dwd, 11, lambda kc, tt: ACT[:, kc, tt * 512:(tt + 1) * 512], lambda kc: [dACT[kc]])
                B.op("dve", lambda e, m=m, P=P: e.scalar_tensor_tensor(out=X[:, m, :], in0=P[:], scalar=0.5,
                                                                       in1=X[:, m, :], op0=ALU.mult, op1=ALU.add),
                     reads=[dP, dX[m]], writes=[dX[m]])

    finals = []
    for c in range(8):
        finals.append(B.dma("pool", xo_d[:, c, :], X[:, c, :], reads=[dX[c]], writes=[dOut], semdep=dX[c]))

    if n_in:
        rmsnorm(n_ffn)
        k = 0
        for i, nco in enumerate(fm_blocks):
            wb, dwb = load_w(win_d[k], 1024)
            k += 1
            P, dP = (PA, dPA) if k % 2 == 0 else (PB, dPB)
            mm_fm(P, dP, wb, dwb, 8, lambda kc, tt: H[:, kc, tt * 512:(tt + 1) * 512], lambda kc: [dHc[kc]], M=nco)
            if fm_dt[i] == "f":
                B.op("act", lambda e, P=P, nco=nco: e.copy(out=SG[:nco, :], in_=P[:nco, :]), reads=[dP], writes=[dSG])
                finals.append(B.dma("sp", fm_out[i], SG[:nco, :], reads=[dSG], writes=[dOut], semdep=dSG))
            else:
                B.op("act", lambda e, P=P, nco=nco: e.copy(out=OSb[:nco, :], in_=P[:nco, :]), reads=[dP], writes=[dOSb])
                finals.append(B.dma("sp", fm_out[i], OSb[:nco, :], reads=[dOSb], writes=[dOut], semdep=dOSb))
        for i, nco in enumerate(tm_blocks):
            wb, dwb = load_w(win_d[k], 1024)
            k += 1
            P, dP = (PA, dPA) if k % 2 == 0 else (PB, dPB)
            for tb in range(16):
                for kc in range(8):
                    B.op("pe", lambda e, tb=tb, kc=kc, P=P, wb=wb, nco=nco: e.matmul(
                        P[:, tb * nco:(tb + 1) * nco], lhsT=H[:, kc, tb * 128:(tb + 1) * 128],
                        rhs=wb[:, kc * 128:kc * 128 + nco], start=(kc == 0), stop=(kc == 7)),
                        reads=[dwb, dHc[kc]], writes=[dP])
            if tm_dt[i] == "f":
                B.op("act", lambda e, P=P, nco=nco: e.copy(out=SG[:, :16 * nco], in_=P[:, :16 * nco]), reads=[dP], writes=[dSG])
                src, dsrc = SG, dSG
            else:
                B.op("act", lambda e, P=P, nco=nco: e.copy(out=OSb[:, :16 * nco], in_=P[:, :16 * nco]), reads=[dP], writes=[dOSb])
                src, dsrc = OSb, dOSb
            finals.append(B.dma("sp", tm_out[i].rearrange("(tb p) c -> p tb c", p=128),
                                src[:, :16 * nco].rearrange("p (tb c) -> p tb c", c=nco),
                                reads=[dsrc], writes=[dOut], semdep=dsrc))
    B.emit(final_waits=finals)
    return nc


def consts_np():
    a = np.arange(128)[:, None]
    b = np.arange(128)[None, :]
    cb = np.zeros((128, 4, 128), np.float32)
    cb[:, 0] = (a == b)
    cb[:, 1] = np.where(a > b, NEG, 0.0)
    cb[:, 2] = np.where(b > a, NEG, 0.0)
    cb[:, 3] = np.where(b >= a, NEG, 0.0)
    cf = np.zeros((128, 4, 128), np.float32)
    cf[:, 0] = (a == b)
    cf[:, 1] = (a <= b)
    cf[:, 2] = (a > b)
    cf[:, 3] = (b < a)
    return cb.astype(ml_dtypes.bfloat16), cf


class Rot:
    def __init__(self, B, n, shape, dt, name, psum=False):
        self.t = [(B.ps if psum else B.sb)(shape, dt, name) for _ in range(n)]
        self.d = [Dep(name + str(i)) for i in range(n)]
        self.n = n

    def __call__(self, i):
        return self.t[i % self.n], self.d[i % self.n]


class RotV:
    def __init__(self, bases, nv, w, name):
        self.t = [b[:, v * w:(v + 1) * w] for b in bases for v in range(nv)]
        self.n = len(self.t)
        self.d = [Dep(name + str(i)) for i in range(self.n)]

    def __call__(self, i):
        return self.t[i % self.n], self.d[i % self.n]


def pipeline(n, stages):
    ns = len(stages)
    for step in range(n + ns - 1):
        for si, st in enumerate(stages):
            i = step - si
            if 0 <= i < n:
                st(i)


def build_sb(NH=4):
    nc = bass.Bass("TRN2", target_bir_lowering=False)
    B = Builder(nc)
    qk_d = nc.dram_tensor("qk", [2 * NH, 128, SEQ], BF16, kind="ExternalInput").ap()
    v_d = nc.dram_tensor("v", [SEQ, NH * 128], BF16, kind="ExternalInput").ap()
    cb_d = nc.dram_tensor("cb", [128, 4, 128], BF16, kind="ExternalInput").ap()
    oT_d = nc.dram_tensor("oT", [NH, 128, SEQ], BF16, kind="ExternalOutput").ap()
    scale = 128 ** -0.5
    QK = B.sb([128, 2 * NH, SEQ], BF16, "QK")
    V = B.sb([128, 32, NH * 128], BF16, "V")
    CB = B.sb([128, 4, 128], BF16, "CB")
    ONESF = B.sb([128, 512], F32, "ONESF")
    ONE1 = B.sb([128, 1], F32, "ONE1")
    ZERO1 = B.sb([128, 1], F32, "ZERO1")
    dQK = [Dep() for _ in range(2 * NH)]
    dV, dCB, dC = Dep(), Dep(), Dep()
    B.dma("sp", CB[:], cb_d, writes=[dCB])
    for h in range(NH):
        B.dma("sp", QK[:, h, :], qk_d[h], writes=[dQK[h]])
        B.dma("pool", QK[:, NH + h, :], qk_d[NH + h], writes=[dQK[NH + h]])
    B.dma("sp", V[:], v_d.rearrange("(blk p) c -> p blk c", p=128), writes=[dV])
    B.op("pool", lambda e: e.memset(ONESF[:], 1.0), writes=[dC])
    B.op("pool", lambda e: e.memset(ONE1[:], 1.0), writes=[dC])
    B.op("pool", lambda e: e.memset(ZERO1[:], 0.0), writes=[dC])
    S = Rot(B, 2, [128, 512], F32, "S", psum=True)
    PT = Rot(B, 2, [128, 512], BF16, "PT", psum=True)
    OT = Rot(B, 2, [128, 128], F32, "OT", psum=True)
    E = Rot(B, 2, [128, 512], F32, "E")
    SP = Rot(B, 2, [128, 512], F32, "SP")
    WT = Rot(B, 2, [128, 512], F32, "WT")
    FF = Rot(B, 2, [128, 512], F32, "FF")
    ARG = Rot(B, 3, [128, 512], F32, "ARG")
    NEGC = Rot(B, 4, [128, 1], F32, "NEGC")
    A = Rot(B, 2, [128, 512], BF16, "A")
    AT = Rot(B, 3, [128, 512], BF16, "AT")
    O = Rot(B, 2, [128, SEQ], BF16, "O")
    tiles = []
    for h in range(NH):
        for qb in range(32):
            kts = list(range(qb // 4, -1, -1))
            for n, kt in enumerate(kts):
                diag = (n == 0)
                W = (qb % 4 + 1) * 128 if diag else 512
                tiles.append((h, qb, kt, W, diag, n == 0, n == len(kts) - 1))
    finals = []

    def st1(i):
        h, qb, kt, W, diag, first, last = tiles[i]
        s, ds = S(i)
        B.op("pe", lambda e: e.matmul(s[:, :W], lhsT=QK[:, h, qb * 128:(qb + 1) * 128],
                                      rhs=QK[:, NH + h, kt * 512:kt * 512 + W], start=True, stop=not diag),
             reads=[dQK[h], dQK[NH + h]], writes=[ds])
        if diag:
            B.op("pe", lambda e: e.matmul(s[:, W - 128:W], lhsT=CB[:, 0, :], rhs=CB[:, 3, :], start=False, stop=True),
                 reads=[dCB], writes=[ds])
        ee, de = E(i)
        sp, dsp = SP(i)
        B.op("act", lambda e: e.activation(out=ee[:, :W], in_=s[:, :W], func=AF.Exp, scale=scale), reads=[ds], writes=[de])
        B.op("act", lambda e: e.activation(out=sp[:, :W], in_=ee[:, :W], func=AF.Ln, bias=ONE1[:], scale=1.0),
             reads=[de, dC], writes=[dsp])
        wt, dwt = WT(i)
        ff, dff = FF(i)
        B.op("dve", lambda e: e.scalar_tensor_tensor(out=wt[:, :W], in0=s[:, :W], scalar=scale, in1=sp[:, :W],
                                                     op0=ALU.mult, op1=ALU.subtract), reads=[ds, dsp], writes=[dwt])
        B.op("dve", lambda e: e.tensor_tensor_scan(out=ff[:, :W], data0=ONESF[:, :W], data1=sp[:, :W], initial=0.0,
                                                   op0=ALU.mult, op1=ALU.add), reads=[dsp, dC], writes=[dff])
        ng, dng = NEGC(i)
        if first:
            B.op("dve", lambda e: e.tensor_tensor(out=ng[:], in0=ZERO1[:], in1=ff[:, W - 1:W], op=ALU.subtract),
                 reads=[dff, dC], writes=[dng])
        else:
            ngo, dngo = NEGC(i - 1)
            B.op("dve", lambda e: e.tensor_tensor(out=ng[:], in0=ngo[:], in1=ff[:, W - 1:W], op=ALU.subtract),
                 reads=[dff, dngo], writes=[dng])
        ar, dar = ARG(i)
        B.op("pool", lambda e: e.tensor_tensor(out=ar[:, :W], in0=wt[:, :W], in1=ff[:, :W], op=ALU.add),
             reads=[dwt, dff], writes=[dar])

    def st2(i):
        h, qb, kt, W, diag, first, last = tiles[i]
        ar, dar = ARG(i)
        ng, dng = NEGC(i)
        a, da = A(i)
        B.op("act", lambda e: e.activation(out=a[:, :W], in_=ar[:, :W], func=AF.Exp, bias=ng[:], scale=1.0),
             reads=[dar, dng], writes=[da])
        pt, dpt = PT(i)
        for j in range(W // 128):
            B.op("pe", lambda e, j=j: e.transpose(out=pt[:, j * 128:(j + 1) * 128], in_=a[:, j * 128:(j + 1) * 128],
                                                  identity=CB[:, 0, :]), reads=[da, dCB], writes=[dpt])
        at, dat = AT(i)
        B.op("dve", lambda e: e.tensor_copy(out=at[:, :W], in_=pt[:, :W]), reads=[dpt], writes=[dat])

    def st3(i):
        h, qb, kt, W, diag, first, last = tiles[i]
        at, dat = AT(i)
        ot, dot = OT(h * 32 + qb)
        nj = W // 128
        for j in range(nj):
            B.op("pe", lambda e, j=j: e.matmul(ot[:], lhsT=V[:, kt * 4 + j, h * 128:(h + 1) * 128],
                                               rhs=at[:, j * 128:(j + 1) * 128], start=(first and j == 0),
                                               stop=(last and j == nj - 1)), reads=[dat, dV], writes=[dot])
        if last:
            o, do = O(h)
            B.op("act", lambda e: e.copy(out=o[:, qb * 128:(qb + 1) * 128], in_=ot[:]), reads=[dot], writes=[do])
            if qb == 31:
                finals.append(B.dma("sp", oT_d[h], o[:], reads=[do], writes=[Dep()], semdep=do))

    pipeline(len(tiles), [st1, st2, st3])
    B.emit(final_waits=finals)
    return nc


def build_fox(NH=2):
    nc = bass.Bass("TRN2", target_bir_lowering=False)
    B = Builder(nc)
    fm_d = nc.dram_tensor("fm", [3 * NH, 128, SEQ], F32, kind="ExternalInput").ap()
    ff_d = nc.dram_tensor("ff", [NH, SEQ], F32, kind="ExternalInput").ap()
    v_d = nc.dram_tensor("v", [SEQ, NH * 128], BF16, kind="ExternalInput").ap()
    par_d = nc.dram_tensor("par", [128, 2 + NH], F32, kind="ExternalInput").ap()
    cb_d = nc.dram_tensor("cb", [128, 4, 128], BF16, kind="ExternalInput").ap()
    oT_d = nc.dram_tensor("oT", [NH, 128, SEQ], BF16, kind="ExternalOutput").ap()
    V = B.sb([128, 32, NH * 128], BF16, "V")
    CB = B.sb([128, 4, 128], BF16, "CB")
    PAR = B.sb([128, 2 + NH], F32, "PAR")
    NB = B.sb([128, NH], F32, "NB")
    ONESF = B.sb([128, 512], F32, "ONESF")
    ONESB = B.sb([128, 128], BF16, "ONESB")
    ONE1 = B.sb([128, 1], F32, "ONE1")
    EPSQ = B.sb([128, 1], F32, "EPSQ")
    EPSK = B.sb([128, 1], F32, "EPSK")
    dV, dCB, dC, dPAR = Dep(), Dep(), Dep(), Dep()
    B.dma("sp", CB[:], cb_d, writes=[dCB])
    B.dma("sp", PAR[:], par_d, writes=[dPAR])
    B.dma("sp", V[:], v_d.rearrange("(blk p) c -> p blk c", p=128), writes=[dV])
    B.op("pool", lambda e: e.memset(ONESF[:], 1.0), writes=[dC])
    B.op("pool", lambda e: e.memset(ONESB[:], 1.0), writes=[dC])
    B.op("pool", lambda e: e.memset(ONE1[:], 1.0), writes=[dC])
    B.op("pool", lambda e: e.memset(EPSQ[:], EPS * 128.0), writes=[dC])
    B.op("pool", lambda e: e.memset(EPSK[:], EPS), writes=[dC])
    B.op("dve", lambda e: e.tensor_scalar(out=NB[:], in0=PAR[:, 2:2 + NH], scalar1=-1.0, scalar2=None, op0=ALU.mult),
         reads=[dPAR], writes=[dPAR])
    RAW = Rot(B, 2, [128, SEQ], F32, "RAW")
    SQ = Rot(B, 1, [128, SEQ], BF16, "SQ")
    QT = Rot(B, 1, [128, SEQ], BF16, "QT")
    KT = Rot(B, 1, [128, SEQ], BF16, "KT")
    GT = Rot(B, 1, [128, SEQ], F32, "GT")
    R1 = Rot(B, 1, [33, SEQ], F32, "R1")
    R2 = Rot(B, 1, [33, SEQ], F32, "R2")
    O = Rot(B, 2, [128, SEQ], BF16, "O")
    PSN = Rot(B, 2, [128, 512], F32, "PSN", psum=True)
    OT = Rot(B, 2, [128, 512], F32, "OT", psum=True)
    DEN = Rot(B, 2, [128, 512], F32, "DEN", psum=True)
    RSt = Rot(B, 2, [128, 512], F32, "RSt")
    PT = Rot(B, 3, [128, 512], BF16, "PT")
    EG = Rot(B, 2, [128, 512], F32, "EG")
    TD = Rot(B, 2, [128, 512], F32, "TD")
    finals = []
    cnt = [0]
    for h in range(NH):
        for which in range(2):
            raw, draw = RAW(cnt[0]); cnt[0] += 1
            B.dma("sp" if which == 0 else "pool", raw[:], fm_d[which * NH + h], writes=[draw])
            sq, dsq = SQ(0)
            B.op("act", lambda e, raw=raw, sq=sq: e.activation(out=sq[:], in_=raw[:], func=AF.Square), reads=[draw], writes=[dsq])
            dst, ddst = (QT(0) if which == 0 else KT(0))
            for tt in range(8):
                ps, dps = PSN(tt)
                sl = slice(tt * 512, (tt + 1) * 512)
                B.op("pe", lambda e, ps=ps, sq=sq, sl=sl: e.matmul(ps[:], lhsT=ONESB[:], rhs=sq[:, sl], start=True, stop=True),
                     reads=[dsq, dC], writes=[dps])
                rs, drs = RSt(tt)
                if which == 0:
                    B.op("act", lambda e, ps=ps, rs=rs: e.activation(out=rs[:], in_=ps[:], func=AF.Sqrt, bias=EPSQ[:], scale=1.0),
                         reads=[dps, dC], writes=[drs])
                else:
                    B.op("act", lambda e, ps=ps, rs=rs: e.activation(out=rs[:], in_=ps[:], func=AF.Sqrt, bias=EPSK[:], scale=1.0 / 128),
                         reads=[dps, dC], writes=[drs])
                B.op("dve", lambda e, rs=rs: e.reciprocal(out=rs[:], in_=rs[:]), reads=[drs], writes=[drs])
                B.op("dve", lambda e, rs=rs, raw=raw, dst=dst, sl=sl, which=which: e.scalar_tensor_tensor(
                    out=dst[:, sl], in0=raw[:, sl], scalar=PAR[:, which:which + 1], in1=rs[:], op0=ALU.mult, op1=ALU.mult),
                    reads=[draw, drs, dPAR], writes=[ddst])
        qt, dqt = QT(0)
        kt, dkt = KT(0)
        gt, dgt = GT(0)
        B.dma("pool", gt[:], fm_d[2 * NH + h], writes=[dgt])
        r1, dr1 = R1(0)
        r2, dr2 = R2(0)
        B.op("pool", lambda e: e.memset(r1[:], 0.0), writes=[dr1])
        B.op("pool", lambda e: e.memset(r2[:], 0.0), writes=[dr2])
        B.dma("sp", r1[32:33, :], ff_d[h:h + 1, :], writes=[dr1])
        B.op("act", lambda e, h=h: e.activation(out=r1[32:33, :], in_=r1[32:33, :], func=AF.Exp, bias=NB[32:33, h:h + 1], scale=-1.0),
             reads=[dr1, dPAR], writes=[dr1])
        B.op("act", lambda e: e.activation(out=r1[32:33, :], in_=r1[32:33, :], func=AF.Ln, bias=ONE1[32:33, :], scale=1.0),
             reads=[dr1, dC], writes=[dr1])
        for tt in range(8):
            sl = slice(tt * 512, (tt + 1) * 512)
            init = 0.0 if tt == 0 else r2[32:33, tt * 512 - 1:tt * 512]
            B.op("dve", lambda e, sl=sl, init=init: e.tensor_tensor_scan(out=r2[32:33, sl], data0=ONESF[32:33, :], data1=r1[32:33, sl],
                                                                         initial=init, op0=ALU.mult, op1=ALU.subtract),
                 reads=[dr1, dC, dr2], writes=[dr2])
        B.dma("sp", r1[0:1, :], r2[32:33, :], reads=[dr2], writes=[dr1])
        B.op("dve", lambda e: e.tensor_scalar(out=r1[0:1, :], in0=r1[0:1, :], scalar1=-1.0, scalar2=None, op0=ALU.mult),
             reads=[dr1], writes=[dr1])
        B.op("pool", lambda e: e.memset(r1[32:33, :], 1.0), reads=[dr2], writes=[dr1])
        B.op("pool", lambda e: e.memset(r2[0:1, :], 1.0), writes=[dr2])
        tiles = []
        for j in range(8):
            nkb = 4 * j + 4
            for kb in range(nkb):
                i = kb - 4 * j
                c0 = 128 * i if i >= 0 else 0
                tiles.append((j, kb, c0, i >= 0, kb == 0, kb == nkb - 1))
        o, do = O(h)

        def st1(t, h=h, qt=qt, kt=kt, r1=r1, r2=r2, dqt=dqt, dkt=dkt, dr1=dr1, dr2=dr2, tiles=tiles):
            j, kb, c0, diag, first, last = tiles[t]
            st, dst_ = PSN(t)
            q0 = j * 512 + c0
            N = 512 - c0
            B.op("pe", lambda e: e.matmul(st[:, c0:512], lhsT=kt[:, kb * 128:(kb + 1) * 128], rhs=qt[:, q0:q0 + N], start=True, stop=False),
                 reads=[dqt, dkt], writes=[dst_])
            B.op("pe", lambda e: e.matmul(st[:, c0:512], lhsT=r1[0:33, kb * 128:(kb + 1) * 128], rhs=r2[0:33, q0:q0 + N], start=False, stop=not diag),
                 reads=[dr1, dr2], writes=[dst_])
            if diag:
                B.op("pe", lambda e: e.matmul(st[:, c0:c0 + 128], lhsT=CB[:, 0, :], rhs=CB[:, 1, :], start=False, stop=True),
                     reads=[dCB], writes=[dst_])
            pt, dpt = PT(t)
            B.op("act", lambda e: e.activation(out=pt[:, c0:512], in_=st[:, c0:512], func=AF.Exp), reads=[dst_], writes=[dpt])

        def st2(t, h=h, o=o, do=do, gt=gt, dgt=dgt, tiles=tiles):
            j, kb, c0, diag, first, last = tiles[t]
            pt, dpt = PT(t)
            ot, dot = OT(j)
            den, dden = DEN(j)
            B.op("pe", lambda e: e.matmul(ot[:, c0:512], lhsT=V[:, kb, h * 128:(h + 1) * 128], rhs=pt[:, c0:512], start=first, stop=last),
                 reads=[dpt, dV], writes=[dot])
            B.op("pe", lambda e: e.matmul(den[:, c0:512], lhsT=ONESB[:], rhs=pt[:, c0:512], start=first, stop=last),
                 reads=[dpt, dC], writes=[dden])
            if last:
                sl = slice(j * 512, (j + 1) * 512)
                eg, deg = EG(j)
                td, dtd = TD(j)
                B.op("act", lambda e: e.activation(out=eg[:], in_=gt[:, sl], func=AF.Exp, scale=-1.0), reads=[dgt], writes=[deg])
                B.op("dve", lambda e: e.scalar_tensor_tensor(out=td[:], in0=eg[:], scalar=1.0, in1=den[:], op0=ALU.add, op1=ALU.mult),
                     reads=[deg, dden], writes=[dtd])
                B.op("dve", lambda e: e.reciprocal(out=td[:], in_=td[:]), reads=[dtd], writes=[dtd])
                B.op("dve", lambda e: e.tensor_tensor(out=o[:, sl], in0=ot[:], in1=td[:], op=ALU.mult), reads=[dot, dtd], writes=[do])

        pipeline(len(tiles), [st1, st2])
        finals.append(B.dma("sp", oT_d[h], o[:], reads=[do], writes=[Dep()], semdep=do))
    B.emit(final_waits=finals)
    return nc


def build_gdn(NH=2, dbg=0):
    nc = bass.Bass("TRN2", target_bir_lowering=False)
    B = Builder(nc)
    fm_d = nc.dram_tensor("fm", [3 * NH, 128, SEQ], F32, kind="ExternalInput").ap()
    gate_d = nc.dram_tensor("gate", [SEQ, NH * 128], F32, kind="ExternalInput").ap()
    ba_d = nc.dram_tensor("ba", [SEQ, 2 * NH], F32, kind="ExternalInput").ap()
    cw_d = nc.dram_tensor("convw", [128, 3 * NH, 4], F32, kind="ExternalInput").ap()
    par_d = nc.dram_tensor("par", [128, 2 * NH], F32, kind="ExternalInput").ap()
    gnb_d = nc.dram_tensor("gnb", [128, 128], F32, kind="ExternalInput").ap()
    cb_d = nc.dram_tensor("cb", [128, 4, 128], BF16, kind="ExternalInput").ap()
    cf_d = nc.dram_tensor("cf", [128, 4, 128], F32, kind="ExternalInput").ap()
    oT_d = nc.dram_tensor("oT", [NH, 128, SEQ], BF16, kind="ExternalOutput").ap()
    CB = B.sb([128, 4, 128], BF16, "CB")
    CF = B.sb([128, 4, 128], F32, "CF")
    CW = B.sb([128, 3 * NH, 4], F32, "CW")
    PAR = B.sb([128, 2 * NH], F32, "PAR")
    GNB = B.sb([128, 128], F32, "GNB")
    BA = B.sb([128, 32, 2 * NH], F32, "BA")
    ONESB = B.sb([128, 128], BF16, "ONESB")
    ONESF = B.sb([128, 128], F32, "ONESF")
    ONE1 = B.sb([128, 1], F32, "ONE1")
    EPS1 = B.sb([128, 1], F32, "EPS1")
    dK = Dep()
    for t, d in ((CB, cb_d), (CF, cf_d), (CW, cw_d), (PAR, par_d), (GNB, gnb_d)):
        B.dma("sp", t[:], d, writes=[dK])
    B.dma("sp", BA[:], ba_d.rearrange("(blk p) c -> p blk c", p=128), writes=[dK])
    B.op("pool", lambda e: e.memset(ONESB[:], 1.0), writes=[dK])
    B.op("pool", lambda e: e.memset(ONESF[:], 1.0), writes=[dK])
    B.op("pool", lambda e: e.memset(ONE1[:], 1.0), writes=[dK])
    B.op("pool", lambda e: e.memset(EPS1[:], EPS), writes=[dK])
    RAW = Rot(B, 2, [128, SEQ + 3], F32, "RAW")
    Y = Rot(B, 1, [128, SEQ], F32, "Y")
    SQ = Rot(B, 1, [128, SEQ], BF16, "SQ")
    QT = Rot(B, 1, [128, SEQ], F32, "QT")
    KT = Rot(B, 1, [128, SEQ], F32, "KT")
    VT = Rot(B, 1, [128, SEQ], F32, "VT")
    SGT = Rot(B, 1, [128, 32, 128], F32, "SGT")
    O = Rot(B, 2, [128, SEQ], BF16, "O")
    PN = Rot(B, 2, [128, 512], F32, "PN", psum=True)
    PQ = RotV([B.ps([128, 512], F32, "PQ") for _ in range(6)], 1, 128, "PQ")
    RSt = Rot(B, 2, [128, 512], F32, "RSt")
    sm = lambda name, n=2: Rot(B, n, [128, 32], F32, name)
    EBt, BETA, EAt, Gt, GC, GL, EGC, EGL, EGLC, BEG = (sm(n) for n in ("EB", "BETA", "EA", "G", "GC", "GL", "EGC", "EGL", "EGLC", "BEG"))
    NEA = Rot(B, 2, [128, 1], F32, "NEA")
    f32t = lambda name, n=2: Rot(B, n, [128, 128], F32, name)
    b16t = lambda name, n=2: Rot(B, n, [128, 128], F32, name)
    KD, KBG, VB, ATT, WTt = b16t("KD", 3), b16t("KBG"), b16t("VB"), b16t("ATT", 3), b16t("WT", 3)
    GM, DEC, DECT, A0, PT32, Ut = f32t("GM"), f32t("DEC"), f32t("DECT"), f32t("A0"), f32t("PT32", 3), f32t("U", 3)
    QP = b16t("QP", 8)
    VN, ON2 = b16t("VN"), b16t("ON2")
    O1, OO, ON, JUNK = f32t("O1"), f32t("OO"), f32t("ON"), f32t("JUNK")
    SS = Rot(B, 2, [128, 1], F32, "SS")
    S32 = Rot(B, 2, [128, 128], F32, "S32")
    finals = []
    ctr = {"raw": 0, "pq": 0, "pqb": 0, "qp": 0}
    hb = {}

    def npq():
        ctr["pq"] += 1
        return PQ(ctr["pq"])

    def npqb():
        return npq()

    def prep(h):
        outs = {}
        for which in range(3):
            raw, draw = RAW(ctr["raw"]); ctr["raw"] += 1
            idx = which * NH + h
            B.op("pool", lambda e, raw=raw: e.memset(raw[:, 0:3], 0.0), writes=[draw])
            B.dma("sp", raw[:, 3:], fm_d[idx], writes=[draw])
            y, dy = Y(0)
            B.op("dve", lambda e, raw=raw, y=y, idx=idx: e.tensor_scalar(out=y[:], in0=raw[:, 0:SEQ], scalar1=CW[:, idx, 0:1], scalar2=None, op0=ALU.mult),
                 reads=[draw, dK], writes=[dy])
            for i in range(1, 4):
                B.op("dve", lambda e, raw=raw, y=y, idx=idx, i=i: e.scalar_tensor_tensor(out=y[:], in0=raw[:, i:i + SEQ], scalar=CW[:, idx, i:i + 1], in1=y[:],
                                                                                       op0=ALU.mult, op1=ALU.add), reads=[draw, dK, dy], writes=[dy])
            if which == 2:
                vt, dvt = VT(h)
                B.op("act", lambda e, y=y, vt=vt: e.activation(out=vt[:], in_=y[:], func=AF.Silu), reads=[dy], writes=[dvt])
                outs["v"] = (vt, dvt)
                continue
            B.op("act", lambda e, y=y: e.activation(out=y[:], in_=y[:], func=AF.Silu), reads=[dy], writes=[dy])
            sq, dsq = SQ(0)
            B.op("act", lambda e, y=y, sq=sq: e.activation(out=sq[:], in_=y[:], func=AF.Square), reads=[dy], writes=[dsq])
            dst, ddst = QT(h) if which == 0 else KT(h)
            for tt in range(8):
                sl = slice(tt * 512, (tt + 1) * 512)
                ps, dps = PN(tt)
                B.op("pe", lambda e, ps=ps, sq=sq, sl=sl: e.matmul(ps[:], lhsT=ONESB[:], rhs=sq[:, sl], start=True, stop=True), reads=[dsq, dK], writes=[dps])
                rs, drs = RSt(tt)
                B.op("act", lambda e, ps=ps, rs=rs: e.activation(out=rs[:], in_=ps[:], func=AF.Sqrt, bias=EPS1[:], scale=1.0), reads=[dps, dK], writes=[drs])
                B.op("dve", lambda e, rs=rs: e.reciprocal(out=rs[:], in_=rs[:]), reads=[drs], writes=[drs])
                cst = 128 ** -0.5 if which == 0 else 1.0
                B.op("dve", lambda e, y=y, rs=rs, dst=dst, sl=sl, cst=cst: e.scalar_tensor_tensor(out=dst[:, sl], in0=y[:, sl], scalar=cst, in1=rs[:],
                                                                                                 op0=ALU.mult, op1=ALU.mult), reads=[dy, drs], writes=[ddst])
            outs["q" if which == 0 else "k"] = (dst, ddst)
        eb, deb = EBt(h); beta, dbeta = BETA(h); ea, dea = EAt(h); g, dg = Gt(h)
        gc, dgc = GC(h); gl, dgl = GL(h); egc, degc = EGC(h); egl, degl = EGL(h); eglc, deglc = EGLC(h); beg, dbeg = BEG(h)
        nea, dnea = NEA(h)
        B.op("act", lambda e: e.activation(out=eb[:], in_=BA[:, :, h], func=AF.Exp, scale=-1.0), reads=[dK], writes=[deb])
        B.op("dve", lambda e: e.tensor_scalar(out=beta[:], in0=eb[:], scalar1=1.0, scalar2=None, op0=ALU.add), reads=[deb], writes=[dbeta])
        B.op("dve", lambda e: e.reciprocal(out=beta[:], in_=beta[:]), reads=[dbeta], writes=[dbeta])
        B.op("act", lambda e: e.activation(out=ea[:], in_=BA[:, :, NH + h], func=AF.Exp, bias=PAR[:, NH + h:NH + h + 1], scale=1.0), reads=[dK], writes=[dea])
        B.op("act", lambda e: e.activation(out=ea[:], in_=ea[:], func=AF.Ln, bias=ONE1[:], scale=1.0), reads=[dea, dK], writes=[dea])
        B.op("act", lambda e: e.activation(out=nea[:], in_=PAR[:, h:h + 1], func=AF.Exp), reads=[dK], writes=[dnea])
        B.op("dve", lambda e: e.tensor_scalar(out=g[:], in0=ea[:], scalar1=nea[:], scalar2=-1.0, op0=ALU.mult, op1=ALU.mult), reads=[dea, dnea], writes=[dg])
        p1, dp1 = npq()
        B.op("pe", lambda e: e.matmul(p1[:, 0:32], lhsT=CF[:, 1, :], rhs=g[:], start=True, stop=True), reads=[dg, dK], writes=[dp1])
        B.op("act", lambda e: e.copy(out=gc[:], in_=p1[:, 0:32]), reads=[dp1], writes=[dgc])
        p2, dp2 = npq()
        B.op("pe", lambda e: e.matmul(p2[:, 0:32], lhsT=ONESF[:], rhs=g[:], start=True, stop=True), reads=[dg, dK], writes=[dp2])
        B.op("act", lambda e: e.copy(out=gl[:], in_=p2[:, 0:32]), reads=[dp2], writes=[dgl])
        B.op("act", lambda e: e.activation(out=egc[:], in_=gc[:], func=AF.Exp), reads=[dgc], writes=[degc])
        B.op("act", lambda e: e.activation(out=egl[:], in_=gl[:], func=AF.Exp), reads=[dgl], writes=[degl])
        B.op("dve", lambda e: e.tensor_tensor(out=eglc[:], in0=gl[:], in1=gc[:], op=ALU.subtract), reads=[dgl, dgc], writes=[deglc])
        B.op("act", lambda e: e.activation(out=eglc[:], in_=eglc[:], func=AF.Exp), reads=[deglc], writes=[deglc])
        B.op("dve", lambda e: e.tensor_tensor(out=beg[:], in0=beta[:], in1=egc[:], op=ALU.mult), reads=[dbeta, degc], writes=[dbeg])
        sgt, dsgt = SGT(h)
        B.dma("pool", sgt[:], gate_d[:, h * 128:(h + 1) * 128].rearrange("(blk p) c -> p blk c", p=128), writes=[dsgt])
        B.op("act", lambda e: e.activation(out=sgt[:], in_=sgt[:], func=AF.Silu), reads=[dsgt], writes=[dsgt])
        s32, ds32 = S32(h); sbb, dsbb = s32, ds32
        B.op("pool", lambda e: e.memset(s32[:], 0.0), writes=[ds32])
        outs.update(beta=(beta, dbeta), g=(g, dg), egc=(egc, degc), egl=(egl, degl), eglc=(eglc, deglc), beg=(beg, dbeg), sgt=(sgt, dsgt),
                    s32=(s32, ds32), sbb=(sbb, dsbb), o=O(h))
        hb[h] = outs

    tiles = [(h, blk) for h in range(NH) for blk in range(32)]

    def mm1(lhsT, rhs, reads, bf=False):
        p, dp = npq()
        B.op("pe", lambda e: e.matmul(p[:], lhsT=lhsT, rhs=rhs, start=True, stop=True), reads=reads, writes=[dp])
        return p, dp

    def tr(src, dsrc):
        p, dp = npqb()
        B.op("pe", lambda e: e.transpose(out=p[:], in_=src, identity=CF[:, 0, :]), reads=[dsrc, dK], writes=[dp])
        return p, dp

    def par(i):
        h, blk = tiles[i]
        hh = hb[h]
        qt, dqt = hh["q"]; kt, dkt = hh["k"]; vt, dvt = hh["v"]
        beta, dbeta = hh["beta"]; g, dg = hh["g"]; eglc, deglc = hh["eglc"]; beg, dbeg = hh["beg"]
        cols = slice(blk * 128, (blk + 1) * 128)
        bc = slice(blk, blk + 1)
        pk, dpk = tr(kt[:, cols], dkt)
        kd, dkd = KD(i); kbg, dkbg = KBG(i); vb, dvb = VB(i)
        B.op("dve", lambda e: e.tensor_scalar(out=kd[:], in0=pk[:], scalar1=eglc[:, bc], scalar2=None, op0=ALU.mult), reads=[dpk, deglc], writes=[dkd])
        B.op("dve", lambda e: e.tensor_scalar(out=kbg[:], in0=pk[:], scalar1=beg[:, bc], scalar2=None, op0=ALU.mult), reads=[dpk, dbeg], writes=[dkbg])
        pv, dpv = tr(vt[:, cols], dvt)
        B.op("dve", lambda e: e.tensor_scalar(out=vb[:], in0=pv[:], scalar1=beta[:, bc], scalar2=None, op0=ALU.mult), reads=[dpv, dbeta], writes=[dvb])
        if dbg == 21:
            return
        pkk, dpkk = mm1(kt[:, cols], kt[:, cols], [dkt])
        pqk, dpqk = mm1(kt[:, cols], qt[:, cols], [dkt, dqt])
        gm, dgm = GM(i)
        B.op("dve", lambda e: e.tensor_scalar(out=gm[:], in0=CF[:, 1, :], scalar1=g[:, bc], scalar2=None, op0=ALU.mult), reads=[dK, dg], writes=[dgm])
        pd, dpd = npq()
        B.op("pe", lambda e: e.matmul(pd[:], lhsT=gm[:], rhs=CF[:, 2, :], start=True, stop=False), reads=[dgm, dK], writes=[dpd])
        B.op("pe", lambda e: e.matmul(pd[:], lhsT=CB[:, 0, :], rhs=CB[:, 2, :], start=False, stop=True), reads=[dK], writes=[dpd])
        pdt, dpdt = npq()
        B.op("pe", lambda e: e.matmul(pdt[:], lhsT=CF[:, 2, :], rhs=gm[:], start=True, stop=False), reads=[dgm, dK], writes=[dpdt])
        B.op("pe", lambda e: e.matmul(pdt[:], lhsT=CB[:, 0, :], rhs=CB[:, 1, :], start=False, stop=True), reads=[dK], writes=[dpdt])
        dec, ddec = DEC(i); dect, ddect = DECT(i)
        B.op("act", lambda e: e.activation(out=dec[:], in_=pd[:], func=AF.Exp), reads=[dpd], writes=[ddec])
        B.op("act", lambda e: e.activation(out=dect[:], in_=pdt[:], func=AF.Exp), reads=[dpdt], writes=[ddect])
        if dbg == 22:
            return
        a0, da0 = A0(i)
        B.op("dve", lambda e: e.scalar_tensor_tensor(out=a0[:], in0=pkk[:], scalar=beta[:, bc], in1=dec[:], op0=ALU.mult, op1=ALU.mult),
             reads=[dpkk, dbeta, ddec], writes=[da0])
        ctr["qp"] += 2
        q0, dq0 = QP(ctr["qp"]); q0t, dq0t = QP(ctr["qp"] + 1)
        B.op("pool", lambda e: e.tensor_tensor(out=q0[:], in0=a0[:], in1=CF[:, 3, :], op=ALU.mult), reads=[da0, dK], writes=[dq0])
        att, datt = ATT(i)
        B.op("dve", lambda e: e.tensor_tensor(out=att[:], in0=pqk[:], in1=dect[:], op=ALU.mult), reads=[dpqk, ddect], writes=[datt])
        if dbg == 23:
            return
        pat, dpat = tr(q0[:], dq0)
        B.op("dve", lambda e: e.tensor_copy(out=q0t[:], in_=pat[:]), reads=[dpat], writes=[dq0t])
        pt32, dpt32 = PT32(i)
        B.op("dve", lambda e: e.tensor_copy(out=pt32[:], in_=pat[:]), reads=[dpat], writes=[dpt32])
        B.op("dve", lambda e: e.tensor_tensor(out=pt32[:], in0=CF[:, 0, :], in1=pt32[:], op=ALU.subtract), reads=[dK, dpt32], writes=[dpt32])
        Q, dQ, Qt, dQt = q0, dq0, q0t, dq0t
        for k in range(1, 7):
            ctr["qp"] += 2
            pq, dpq = mm1(Qt[:], Q[:], [dQ, dQt])
            qn, dqn = QP(ctr["qp"])
            B.op("act", lambda e, qn=qn, pq=pq: e.copy(out=qn[:], in_=pq[:]), reads=[dpq], writes=[dqn])
            if k < 6:
                pqt, dpqt = mm1(Q[:], Qt[:], [dQ, dQt])
                qtn, dqtn = QP(ctr["qp"] + 1)
                B.op("dve", lambda e, qtn=qtn, pqt=pqt: e.tensor_copy(out=qtn[:], in_=pqt[:]), reads=[dpqt], writes=[dqtn])
            pp, dpp = mm1(qn[:], pt32[:], [dqn, dpt32])
            B.op("dve", lambda e, pp=pp: e.tensor_tensor(out=pt32[:], in0=pt32[:], in1=pp[:], op=ALU.add), reads=[dpp, dpt32], writes=[dpt32])
            if k < 6:
                Q, dQ, Qt, dQt = qn, dqn, qtn, dqtn
        ptb, dptb = pt32, dpt32
        pu, dpu = mm1(ptb[:], vb[:], [dptb, dvb])
        u, du = Ut(i)
        B.op("act", lambda e: e.copy(out=u[:], in_=pu[:]), reads=[dpu], writes=[du])
        pw, dpw = mm1(kbg[:], ptb[:], [dkbg, dptb])
        wt, dwt = WTt(i)
        B.op("dve", lambda e: e.tensor_copy(out=wt[:], in_=pw[:]), reads=[dpw], writes=[dwt])

    def seq(i):
        h, blk = tiles[i]
        hh = hb[h]
        qt, dqt = hh["q"]; egc, degc = hh["egc"]; egl, degl = hh["egl"]; sgt, dsgt = hh["sgt"]
        s32, ds32 = hh["s32"]; sbb, dsbb = hh["sbb"]; o, do = hh["o"]
        cols = slice(blk * 128, (blk + 1) * 128)
        bc = slice(blk, blk + 1)
        u, du = Ut(i); wt, dwt = WTt(i); att, datt = ATT(i); kd, dkd = KD(i)
        ps1, dps1 = mm1(wt[:], sbb[:], [dwt, dsbb])
        ps2, dps2 = mm1(qt[:, cols], sbb[:], [dqt, dsbb])
        vn, dvn = VN(i)
        B.op("dve", lambda e: e.tensor_tensor(out=vn[:], in0=u[:], in1=ps1[:], op=ALU.subtract), reads=[du, dps1], writes=[dvn])
        o1, do1 = O1(i)
        B.op("dve", lambda e: e.tensor_scalar(out=o1[:], in0=ps2[:], scalar1=egc[:, bc], scalar2=None, op0=ALU.mult), reads=[dps2, degc], writes=[do1])
        ps3, dps3 = mm1(att[:], vn[:], [datt, dvn])
        ps4, dps4 = mm1(kd[:], vn[:], [dkd, dvn])
        B.op("dve", lambda e: e.scalar_tensor_tensor(out=s32[:], in0=s32[:], scalar=egl[:, bc], in1=ps4[:], op0=ALU.mult, op1=ALU.add),
             reads=[dps4, degl, ds32], writes=[ds32])
        oo, doo = OO(i)
        B.op("dve", lambda e: e.tensor_tensor(out=oo[:], in0=o1[:], in1=ps3[:], op=ALU.add), reads=[do1, dps3], writes=[doo])
        junk, djunk = JUNK(i); ss, dss = SS(i)
        B.op("act", lambda e: e.activation(out=junk[:], in_=oo[:], func=AF.Square, accum_out=ss[:]), reads=[doo], writes=[djunk, dss])
        B.op("act", lambda e: e.activation(out=ss[:], in_=ss[:], func=AF.Sqrt, bias=EPS1[:], scale=1.0 / 128), reads=[dss, dK], writes=[dss])
        B.op("dve", lambda e: e.reciprocal(out=ss[:], in_=ss[:]), reads=[dss], writes=[dss])
        on, don = ON(i)
        B.op("dve", lambda e: e.scalar_tensor_tensor(out=on[:], in0=oo[:], scalar=ss[:], in1=GNB[:], op0=ALU.mult, op1=ALU.mult),
             reads=[doo, dss, dK], writes=[don])
        on2, don2 = ON2(i)
        B.op("pool", lambda e: e.tensor_tensor(out=on2[:], in0=on[:], in1=sgt[:, blk, :], op=ALU.mult), reads=[don, dsgt], writes=[don2])
        po, dpo = tr(on2[:], don2)
        B.op("dve", lambda e: e.tensor_copy(out=o[:, cols], in_=po[:]), reads=[dpo], writes=[do])
        if blk == 31:
            finals.append(B.dma("sp", oT_d[h], o[:], reads=[do], writes=[Dep()], semdep=do))

    if dbg == 1:
        for h in range(NH):
            prep(h)
            finals.append(B.dma("sp", oT_d[h], hb[h]["q"][0][:], reads=[hb[h]["q"][1]], writes=[Dep()], semdep=hb[h]["q"][1]))
    elif dbg >= 2:
        for i in range(len(tiles)):
            if tiles[i][1] == 0:
                prep(tiles[i][0])
            par(i)
        for h in range(NH):
            finals.append(B.dma("sp", oT_d[h], hb[h]["q"][0][:], reads=[hb[h]["q"][1]], writes=[Dep()], semdep=hb[h]["q"][1]))
    else:
        n = len(tiles)
        for step in range(n + 1):
            if step < n and tiles[step][1] == 0:
                if step > 0:
                    seq(step - 1)
                prep(tiles[step][0])
                par(step)
            else:
                if step < n:
                    par(step)
                seq(step - 1)
    B.emit(final_waits=finals)
    return nc


NCORES = 8
_CACHE = {}
_DBG = None


def _prog(key, fn):
    if key not in _CACHE:
        _CACHE[key] = fn()
    return _CACHE[key]


def _launch(nc, in_maps):
    res = run_bass_kernel_spmd(nc, in_maps, core_ids=list(range(NCORES)))
    return res.results


def _colblock(W, cols):
    blk = np.zeros((W.shape[0], 128), np.float32)
    blk[:, :len(cols)] = W[:, cols]
    return wblocks(blk)[0]


def _even_blocks(W):
    fm_cols, tm_cols, fm_dt, tm_dt = [], [], [], []
    r128 = lambda base, h: list(range(base + h * 128, base + (h + 1) * 128))
    for r in range(2):
        hs = (2 * r, 2 * r + 1)
        for base in (0, 512, 1024, 2056, 2568, 3592):
            for h in hs:
                fm_cols.append(r128(base, h)); fm_dt.append("f")
        fm_cols.append([4104 + hs[0], 4104 + hs[1]]); fm_dt.append("f")
        for h in hs:
            tm_cols.append(r128(1536, h)); tm_dt.append("f")
        tm_cols.append([2048 + hs[0], 2048 + hs[1], 2052 + hs[0], 2052 + hs[1]]); tm_dt.append("f")
        for h in hs:
            tm_cols.append(r128(3080, h)); tm_dt.append("b")
    win = np.stack([_colblock(W, c) for c in fm_cols + tm_cols])
    return [len(c) for c in fm_cols], [len(c) for c in tm_cols], fm_dt, tm_dt, win


def _odd_blocks(W):
    fm_cols = [list(range(i * 128, (i + 1) * 128)) for i in range(16)]
    tm_cols = [list(range(2048 + i * 128, 2048 + (i + 1) * 128)) for i in range(8)]
    win = np.stack([_colblock(W, c) for c in fm_cols + tm_cols])
    return [128] * 16, [128] * 8, ["b"] * 16, ["b"] * 8, win


def kernel(x, norm_ffn1, ffn1_w_gu, ffn1_w_down, norm_mix, w_in_even, dn_conv_w, dn_a_log, dn_dt_bias, dn_norm_g,
           fox_q_norm_g, fox_k_norm_g, fox_f_bias, w_out_even, w_in_odd, w_out_odd, norm_ffn2, ffn2_w_gu, ffn2_w_down):
    f = lambda a: np.ascontiguousarray(np.asarray(a, dtype=np.float32))
    x = f(x)
    cb, cf = consts_np()
    cores = [(b, r) for b in range(BATCH) for r in range(2)]
    xT = [xT_lay(x[b, r * TOK:(r + 1) * TOK]) for (b, r) in cores]
    oT_in = None
    cat = lambda arrs, ax: np.ascontiguousarray(np.concatenate(arrs, axis=ax))
    for p in range(DEPTH + 1):
        has_out = p > 0
        ffns = []
        if p > 0:
            ffns.append((norm_ffn2[p - 1], ffn2_w_gu[p - 1], ffn2_w_down[p - 1]))
        if p < DEPTH:
            ffns.append((norm_ffn1[p], ffn1_w_gu[p], ffn1_w_down[p]))
        gains = [gain_lay(f(g)) for g, _, _ in ffns]
        wgu = np.concatenate([gu_blocks(f(w)) for _, w, _ in ffns], 0)
        wdn = np.concatenate([dn_blocks(f(w)).reshape(16, 128, 1408) for _, _, w in ffns], 0)
        if p < DEPTH:
            even = (p % 2 == 0)
            j = p // 2
            fmb, tmb, fmd, tmd, win = _even_blocks(f(w_in_even[j])) if even else _odd_blocks(f(w_in_odd[j]))
            gains.append(gain_lay(f(norm_mix[p])))
        else:
            fmb, tmb, fmd, tmd, win = [], [], [], [], None
        key = ("tl", has_out, len(ffns), tuple(fmb), tuple(tmb), tuple(fmd), tuple(tmd))
        nc = _prog(key, lambda: build_tl(has_out, len(ffns), fmb, tmb, fmd, tmd))
        common = {"gains": cat(gains, 1), "wgu": wgu, "wdn": wdn}
        if win is not None:
            common["win"] = win
        if has_out:
            wo = f(w_out_even[(p - 1) // 2]) if (p - 1) % 2 == 0 else f(w_out_odd[(p - 1) // 2])
            common["wout"] = wblocks(wo)
        in_maps = []
        for ci in range(NCORES):
            m = dict(common)
            m["xT"] = xT[ci]
            if has_out:
                m["oT"] = oT_in[ci]
            in_maps.append(m)
        res = _launch(nc, in_maps)
        xT = [res[ci]["xo"] for ci in range(NCORES)]
        if _DBG is not None:
            _DBG["x_p%d" % p] = [np.array(a) for a in xT]
        if p == DEPTH:
            break
        fm = lambda ci, i: res[ci]["fm%d" % i]
        tm = lambda ci, i: res[ci]["tm%d" % i]
        if even:
            g_maps, f_maps = [], []
            for (b, r) in cores:
                c0, c1 = 2 * b, 2 * b + 1
                seqcat = lambda i: cat([fm(c0, r * 13 + i), fm(c1, r * 13 + i)], 1)
                tmcat = lambda i: cat([tm(c0, r * 5 + i), tm(c1, r * 5 + i)], 0)
                hs = (2 * r, 2 * r + 1)
                cw = f(dn_conv_w[j]).reshape(4, 3, 4, 128)[:, :, hs[0]:hs[1] + 1, :].transpose(3, 1, 2, 0).reshape(128, 6, 4)
                par = np.empty((128, 4), np.float32)
                par[:, 0:2] = f(dn_a_log[j])[None, hs[0]:hs[1] + 1]
                par[:, 2:4] = f(dn_dt_bias[j])[None, hs[0]:hs[1] + 1]
                g_maps.append({"fm": np.stack([seqcat(i) for i in range(6)]), "gate": cat([tmcat(0), tmcat(1)], 1), "ba": tmcat(2),
                               "convw": np.ascontiguousarray(cw), "par": par,
                               "gnb": np.ascontiguousarray(np.broadcast_to(f(dn_norm_g[j])[None, :], (128, 128))), "cb": cb, "cf": cf})
                fpar = np.empty((128, 4), np.float32)
                fpar[:, 0] = f(fox_q_norm_g[j]); fpar[:, 1] = f(fox_k_norm_g[j])
                fpar[:, 2:4] = f(fox_f_bias[j])[None, hs[0]:hs[1] + 1]
                f_maps.append({"fm": np.stack([seqcat(i) for i in range(6, 12)]), "ff": seqcat(12), "v": cat([tmcat(3), tmcat(4)], 1),
                               "par": fpar, "cb": cb})
            gres = _launch(_prog("gdn", lambda: build_gdn(2)), g_maps)
            fres = _launch(_prog("fox", lambda: build_fox(2)), f_maps)
            oT_in = []
            for (b, r) in cores:
                sl = slice(r * TOK, (r + 1) * TOK)
                ch = [gres[2 * b + h // 2]["oT"][h % 2][:, sl] for h in range(4)] + [fres[2 * b + h // 2]["oT"][h % 2][:, sl] for h in range(4)]
                oT_in.append(np.ascontiguousarray(np.stack(ch)))
        else:
            s_maps = []
            for (b, r) in cores:
                c0, c1 = 2 * b, 2 * b + 1
                seqcat = lambda i: cat([fm(c0, i), fm(c1, i)], 1)
                qk = np.stack([seqcat(4 * r + i) for i in range(4)] + [seqcat(8 + 4 * r + i) for i in range(4)])
                v = cat([cat([tm(c0, 4 * r + i) for i in range(4)], 1), cat([tm(c1, 4 * r + i) for i in range(4)], 1)], 0)
                s_maps.append({"qk": np.ascontiguousarray(qk), "v": v, "cb": cb})
            sres = _launch(_prog("sb", lambda: build_sb(4)), s_maps)
            oT_in = []
            for (b, r) in cores:
                sl = slice(r * TOK, (r + 1) * TOK)
                oT_in.append(np.ascontiguousarray(np.stack([sres[2 * b + h // 4]["oT"][h % 4][:, sl] for h in range(8)])))
        if _DBG is not None:
            _DBG["o_l%d" % p] = [np.array(a) for a in oT_in]
    out = np.empty((BATCH, SEQ, D_MODEL), np.float32)
    for ci, (b, r) in enumerate(cores):
        out[b, r * TOK:(r + 1) * TOK] = xT_unlay(np.asarray(xT[ci], dtype=np.float32))
    return out
```
